# Optimizing a Trainium2 kernel written in Bass

```python
import math
import jax, jax.numpy as jnp
from jax import lax
import numpy as np


D_MODEL = 2048
BATCH = 2
SEQ = 4096
DEPTH = 4

N_MIXERS = 4
HEAD_DIM = 128
N_SB_HEADS = D_MODEL // HEAD_DIM
N_FOX_HEADS = D_MODEL // HEAD_DIM
SWA_HEAD_DIM = 64
N_SWA_HEADS = D_MODEL // SWA_HEAD_DIM
N_SWA_KV_HEADS = N_SWA_HEADS // 8
WINDOW = 128
MLA_HEADS = D_MODEL // 128
Q_LORA = D_MODEL // 4
KV_LORA = D_MODEL // 8
QK_NOPE = 128
QK_ROPE = 64
V_HEAD = 128
ROPE_THETA = 10000.0
MEM_LEN = 256
MEM_HEADS = 4
MEM_HEAD_DIM = 128
D_FF = ((8 * D_MODEL // 3 + 255) // 256) * 256
CONV_WIDTH = 3
NUM_BUCKETS = 32
MAX_DISTANCE = 128
Q_BLOCK = 128
EPS = 1e-6

SB_W = N_SB_HEADS * HEAD_DIM
FOX_W = N_FOX_HEADS * HEAD_DIM
SWA_W = N_SWA_HEADS * SWA_HEAD_DIM
SWA_KV_W = N_SWA_KV_HEADS * SWA_HEAD_DIM
MLA_W = MLA_HEADS * V_HEAD
MEM_W = MEM_HEADS * MEM_HEAD_DIM

kernel_name = 'hybrid_interleaved_sb_fox_swa_mla'


def _n_uses(m):
    return (DEPTH - m + N_MIXERS - 1) // N_MIXERS


def _rmsnorm(x, g):
    x32 = x.astype(jnp.float32)
    y = x32 * lax.rsqrt(jnp.mean(x32 * x32, axis=-1, keepdims=True) + EPS)
    return y.astype(x.dtype) * g


def _to_blocks(a):
    B, H, S = a.shape[:3]
    a = a.reshape((B, H, S // Q_BLOCK, Q_BLOCK) + a.shape[3:])
    return jnp.moveaxis(a, 2, 0)


def _from_blocks(a):
    nb, B, H, qb = a.shape[:4]
    return jnp.moveaxis(a, 0, 2).reshape((B, H, nb * qb) + a.shape[4:])


def _stick_breaking_attention(q, k, v):
    S = q.shape[2]
    scale = q.shape[-1] ** -0.5
    kpos = jnp.arange(S)

    def block(xs):
        qb, i = xs
        qpos = i * Q_BLOCK + jnp.arange(Q_BLOCK)
        z = jnp.einsum('bhqd,bhkd->bhqk', qb, k).astype(jnp.float32) * scale
        strict = kpos[None, :] < qpos[:, None]
        log_keep = jnp.where(strict, jax.nn.log_sigmoid(-z), 0.0)
        log_after = lax.cumsum(log_keep, axis=3, reverse=True) - log_keep
        a = jnp.where(strict, jnp.exp(jax.nn.log_sigmoid(z) + log_after), 0.0)
        return jnp.einsum('bhqk,bhkd->bhqd', a.astype(v.dtype), v)

    out = lax.map(block, (_to_blocks(q), jnp.arange(S // Q_BLOCK)))
    return _from_blocks(out)


def _causal_softmax_attention(q, k, v, log_fcum=None):
    S = q.shape[2]
    scale = q.shape[-1] ** -0.5
    kpos = jnp.arange(S)
    nb = S // Q_BLOCK

    def block(xs):
        qb, i = xs[0], xs[1]
        qpos = i * Q_BLOCK + jnp.arange(Q_BLOCK)
        logits = jnp.einsum('bhqd,bhkd->bhqk', qb, k).astype(jnp.float32) * scale
        if log_fcum is not None:
            logits = logits + (xs[2][..., :, None] - log_fcum[:, :, None, :])
        logits = jnp.where(kpos[None, :] <= qpos[:, None], logits, -jnp.inf)
        p = jax.nn.softmax(logits, axis=-1)
        return jnp.einsum('bhqk,bhkd->bhqd', p.astype(v.dtype), v)

    if log_fcum is None:
        xs = (_to_blocks(q), jnp.arange(nb))
    else:
        xs = (_to_blocks(q), jnp.arange(nb), _to_blocks(log_fcum))
    return _from_blocks(lax.map(block, xs))


def _memory_attention(q_mem, mem_h, w_mem_kv):
    B, S, _ = q_mem.shape
    q = q_mem.reshape(B, S, MEM_HEADS, MEM_HEAD_DIM)
    kv = (mem_h @ w_mem_kv).reshape(B, mem_h.shape[1], 2, MEM_HEADS, MEM_HEAD_DIM)
    logits = jnp.einsum('bshd,bmhd->bhsm', q, kv[:, :, 0]).astype(jnp.float32) * MEM_HEAD_DIM ** -0.5
    p = jax.nn.softmax(logits, axis=-1)
    o = jnp.einsum('bhsm,bmhd->bshd', p.astype(q.dtype), kv[:, :, 1])
    return o.reshape(B, S, MEM_W)


def _heads(t, n, d):
    B, S, _ = t.shape
    return t.reshape(B, S, n, d).transpose(0, 2, 1, 3)


def _merge_heads(t):
    B, H, S, d = t.shape
    return t.transpose(0, 2, 1, 3).reshape(B, S, H * d)


def _sb_mixer(h, mem_h, w_in, w_mem_kv, w_out):
    q, k, v, q_mem = jnp.split(h @ w_in, [SB_W, 2 * SB_W, 3 * SB_W], axis=-1)
    o = _stick_breaking_attention(_heads(q, N_SB_HEADS, HEAD_DIM), _heads(k, N_SB_HEADS, HEAD_DIM),
                                  _heads(v, N_SB_HEADS, HEAD_DIM))
    o_mem = _memory_attention(q_mem, mem_h, w_mem_kv)
    return jnp.concatenate([_merge_heads(o), o_mem], axis=-1) @ w_out


def _fox_mixer(h, mem_h, w_in, b_f, w_mem_kv, w_out):
    q, k, v, f_logit, q_mem = jnp.split(
        h @ w_in, [FOX_W, 2 * FOX_W, 3 * FOX_W, 3 * FOX_W + N_FOX_HEADS], axis=-1)
    log_f = jax.nn.log_sigmoid((f_logit + b_f).astype(jnp.float32))
    log_fcum = jnp.cumsum(log_f, axis=1).transpose(0, 2, 1)
    o = _causal_softmax_attention(_heads(q, N_FOX_HEADS, HEAD_DIM), _heads(k, N_FOX_HEADS, HEAD_DIM),
                                  _heads(v, N_FOX_HEADS, HEAD_DIM), log_fcum)
    o_mem = _memory_attention(q_mem, mem_h, w_mem_kv)
    return jnp.concatenate([_merge_heads(o), o_mem], axis=-1) @ w_out


def _t5_bucket(dist):
    max_exact = NUM_BUCKETS // 2
    d = jnp.maximum(dist, 1).astype(jnp.float32)
    large = max_exact + (jnp.log(d / max_exact) / math.log(MAX_DISTANCE / max_exact)
                         * (NUM_BUCKETS - max_exact)).astype(jnp.int32)
    return jnp.where(dist < max_exact, dist, jnp.minimum(large, NUM_BUCKETS - 1))


def _swa_mixer(h, mem_h, rel_bias, w_in, sinks, w_mem_kv, w_out):
    B, S, _ = h.shape
    q, k, v, q_mem = jnp.split(h @ w_in, [SWA_W, SWA_W + SWA_KV_W, SWA_W + 2 * SWA_KV_W], axis=-1)
    nb = S // WINDOW
    G = N_SWA_HEADS // N_SWA_KV_HEADS
    qb = q.reshape(B, nb, WINDOW, N_SWA_KV_HEADS, G, SWA_HEAD_DIM)

    def window_keys(t):
        t = jnp.pad(t.reshape(B, S, N_SWA_KV_HEADS, SWA_HEAD_DIM), ((0, 0), (WINDOW, 0), (0, 0), (0, 0)))
        t = t.reshape(B, nb + 1, WINDOW, N_SWA_KV_HEADS, SWA_HEAD_DIM)
        return jnp.concatenate([t[:, :-1], t[:, 1:]], axis=2)

    kw, vw = window_keys(k), window_keys(v)
    logits = jnp.einsum('bnqhgd,bnkhd->bnhgqk', qb, kw).astype(jnp.float32) * SWA_HEAD_DIM ** -0.5
    qi = jnp.arange(WINDOW)[:, None]
    kj = jnp.arange(2 * WINDOW)[None, :]
    dist = WINDOW + qi - kj
    band = (dist >= 0) & (dist < WINDOW)
    real = (jnp.arange(nb)[:, None, None] * WINDOW + kj[None] - WINDOW) >= 0
    mask = band[None] & real
    bias = rel_bias[_t5_bucket(jnp.maximum(dist, 0))]
    bias = bias.transpose(2, 0, 1).reshape(N_SWA_KV_HEADS, G, WINDOW, 2 * WINDOW)
    logits = jnp.where(mask[None, :, None, None], logits + bias.astype(jnp.float32), -jnp.inf)
    sink = sinks.reshape(N_SWA_KV_HEADS, G)[None, None, :, :, None, None].astype(jnp.float32)
    m = jnp.maximum(jnp.max(logits, axis=-1, keepdims=True), sink)
    p = jnp.exp(logits - m)
    w = p / (jnp.sum(p, axis=-1, keepdims=True) + jnp.exp(sink - m))
    o = jnp.einsum('bnhgqk,bnkhd->bnqhgd', w.astype(vw.dtype), vw).reshape(B, S, SWA_W)
    o_mem = _memory_attention(q_mem, mem_h, w_mem_kv)
    return jnp.concatenate([o, o_mem], axis=-1) @ w_out


def _rope(x, positions):
    half = x.shape[-1] // 2
    inv_freq = ROPE_THETA ** (-jnp.arange(half, dtype=jnp.float32) / half)
    ang = positions.astype(jnp.float32)[:, :, None, None] * inv_freq
    cos, sin = jnp.cos(ang).astype(x.dtype), jnp.sin(ang).astype(x.dtype)
    x1, x2 = x[..., :half], x[..., half:]
    return jnp.concatenate([x1 * cos - x2 * sin, x2 * cos + x1 * sin], axis=-1)


def _mla_mixer(h, mem_h, positions, w_in, q_norm, w_uq, kv_norm, w_ukv, w_mem_kv, w_out):
    B, S, _ = h.shape
    c_q, c_kv, k_rope, q_mem = jnp.split(
        h @ w_in, [Q_LORA, Q_LORA + KV_LORA, Q_LORA + KV_LORA + QK_ROPE], axis=-1)
    q = (_rmsnorm(c_q, q_norm) @ w_uq).reshape(B, S, MLA_HEADS, QK_NOPE + QK_ROPE)
    kv = (_rmsnorm(c_kv, kv_norm) @ w_ukv).reshape(B, S, MLA_HEADS, QK_NOPE + V_HEAD)
    q_nope, q_pe = jnp.split(q, [QK_NOPE], axis=-1)
    k_nope, v = jnp.split(kv, [QK_NOPE], axis=-1)
    q_pe = _rope(q_pe, positions)
    k_pe = _rope(k_rope[:, :, None, :], positions)
    q = jnp.concatenate([q_nope, q_pe], axis=-1)
    k = jnp.concatenate([k_nope, jnp.broadcast_to(k_pe, (B, S, MLA_HEADS, QK_ROPE))], axis=-1)
    o = _causal_softmax_attention(q.transpose(0, 2, 1, 3), k.transpose(0, 2, 1, 3), v.transpose(0, 2, 1, 3))
    o_mem = _memory_attention(q_mem, mem_h, w_mem_kv)
    return jnp.concatenate([_merge_heads(o), o_mem], axis=-1) @ w_out


def _conv_ffn(h, w_up, conv_w, conv_b, w_down):
    S = h.shape[1]
    u = h @ w_up
    up = jnp.pad(u, ((0, 0), (CONV_WIDTH - 1, 0), (0, 0)))
    c = conv_b
    for tap in range(CONV_WIDTH):
        c = c + conv_w[tap] * up[:, tap:tap + S]
    gate, val = jnp.split(c, [D_FF], axis=-1)
    return (jax.nn.silu(gate) * val) @ w_down


def setup_inputs(seed: int = 0) -> dict:
    key = jax.random.key(seed)
    ks = iter(jax.random.split(key, 32))
    f32 = jnp.float32

    def w(shape, fan_in):
        return jax.random.normal(next(ks), shape, f32) * fan_in ** -0.5

    def gain(shape):
        return 1.0 + 0.05 * jax.random.normal(next(ks), shape, f32)

    na, nb, nc, nd = _n_uses(0), _n_uses(1), _n_uses(2), _n_uses(3)
    x = jax.random.normal(next(ks), (BATCH, SEQ, D_MODEL), f32)
    mem = jax.random.normal(next(ks), (BATCH, MEM_LEN, D_MODEL), f32)
    positions = (jax.random.randint(next(ks), (BATCH, 1), 0, 1024, dtype=jnp.int32)
                 + jnp.arange(SEQ, dtype=jnp.int32)[None, :])
    rel_bias = 0.5 * jax.random.normal(next(ks), (NUM_BUCKETS, N_SWA_HEADS), f32)
    attn_norm = gain((DEPTH, D_MODEL))
    mem_norm = gain((DEPTH, D_MODEL))
    w_mem_kv = w((DEPTH, D_MODEL, 2 * MEM_W), D_MODEL)
    ffn_norm = gain((DEPTH, D_MODEL))
    ffn_w_up = w((DEPTH, D_MODEL, 2 * D_FF), D_MODEL)
    ffn_conv_w = w((DEPTH, CONV_WIDTH, 2 * D_FF), CONV_WIDTH)
    ffn_conv_b = 0.02 * jax.random.normal(next(ks), (DEPTH, 2 * D_FF), f32)
    ffn_w_down = w((DEPTH, D_FF, D_MODEL), D_FF)
    final_norm = gain((D_MODEL,))
    sb_w_in = w((na, D_MODEL, 3 * SB_W + MEM_W), D_MODEL)
    sb_w_out = w((na, SB_W + MEM_W, D_MODEL), SB_W + MEM_W)
    fox_w_in = w((nb, D_MODEL, 3 * FOX_W + N_FOX_HEADS + MEM_W), D_MODEL)
    fox_b_f = 2.0 + 0.1 * jax.random.normal(next(ks), (nb, N_FOX_HEADS), f32)
    fox_w_out = w((nb, FOX_W + MEM_W, D_MODEL), FOX_W + MEM_W)
    swa_w_in = w((nc, D_MODEL, SWA_W + 2 * SWA_KV_W + MEM_W), D_MODEL)
    swa_sinks = 0.5 * jax.random.normal(next(ks), (nc, N_SWA_HEADS), f32)
    swa_w_out = w((nc, SWA_W + MEM_W, D_MODEL), SWA_W + MEM_W)
    mla_w_in = w((nd, D_MODEL, Q_LORA + KV_LORA + QK_ROPE + MEM_W), D_MODEL)
    mla_q_norm = gain((nd, Q_LORA))
    mla_w_uq = w((nd, Q_LORA, MLA_HEADS * (QK_NOPE + QK_ROPE)), Q_LORA)
    mla_kv_norm = gain((nd, KV_LORA))
    mla_w_ukv = w((nd, KV_LORA, MLA_HEADS * (QK_NOPE + V_HEAD)), KV_LORA)
    mla_w_out = w((nd, MLA_W + MEM_W, D_MODEL), MLA_W + MEM_W)
    return {'x': x, 'mem': mem, 'positions': positions, 'rel_bias': rel_bias,
            'attn_norm': attn_norm, 'mem_norm': mem_norm, 'w_mem_kv': w_mem_kv,
            'ffn_norm': ffn_norm, 'ffn_w_up': ffn_w_up, 'ffn_conv_w': ffn_conv_w,
            'ffn_conv_b': ffn_conv_b, 'ffn_w_down': ffn_w_down, 'final_norm': final_norm,
            'sb_w_in': sb_w_in, 'sb_w_out': sb_w_out,
            'fox_w_in': fox_w_in, 'fox_b_f': fox_b_f, 'fox_w_out': fox_w_out,
            'swa_w_in': swa_w_in, 'swa_sinks': swa_sinks, 'swa_w_out': swa_w_out,
            'mla_w_in': mla_w_in, 'mla_q_norm': mla_q_norm, 'mla_w_uq': mla_w_uq,
            'mla_kv_norm': mla_kv_norm, 'mla_w_ukv': mla_w_ukv, 'mla_w_out': mla_w_out}


def reference(x, mem, positions, rel_bias, attn_norm, mem_norm, w_mem_kv, ffn_norm, ffn_w_up,
              ffn_conv_w, ffn_conv_b, ffn_w_down, final_norm, sb_w_in, sb_w_out,
              fox_w_in, fox_b_f, fox_w_out, swa_w_in, swa_sinks, swa_w_out,
              mla_w_in, mla_q_norm, mla_w_uq, mla_kv_norm, mla_w_ukv, mla_w_out):
    for i in range(DEPTH):
        kind, j = i % N_MIXERS, i // N_MIXERS
        h = _rmsnorm(x, attn_norm[i])
        mem_h = _rmsnorm(mem, mem_norm[i])
        if kind == 0:
            y = _sb_mixer(h, mem_h, sb_w_in[j], w_mem_kv[i], sb_w_out[j])
        elif kind == 1:
            y = _fox_mixer(h, mem_h, fox_w_in[j], fox_b_f[j], w_mem_kv[i], fox_w_out[j])
        elif kind == 2:
            y = _swa_mixer(h, mem_h, rel_bias, swa_w_in[j], swa_sinks[j], w_mem_kv[i], swa_w_out[j])
        else:
            y = _mla_mixer(h, mem_h, positions, mla_w_in[j], mla_q_norm[j], mla_w_uq[j],
                           mla_kv_norm[j], mla_w_ukv[j], w_mem_kv[i], mla_w_out[j])
        x = x + y
        x = x + _conv_ffn(_rmsnorm(x, ffn_norm[i]), ffn_w_up[i], ffn_conv_w[i], ffn_conv_b[i], ffn_w_down[i])
    return _rmsnorm(x, final_norm)
```

```python
import os
import numpy as np
import contextlib
import concourse.bass as bass
import concourse.mybir as mybir
from concourse.bass_utils import run_bass_kernel_spmd

F32 = mybir.dt.float32
BF16 = mybir.dt.bfloat16
I32 = mybir.dt.int32
AF = mybir.ActivationFunctionType
ALU = mybir.AluOpType
AX = mybir.AxisListType

D = 2048
SEQ = 4096
TOK = 1024
NCH = 16
DFF = 5632
NFF = 44
EPS = 1e-6
NEG = -30000.0
GROUPS = [[0, 1, 2, 3], [4, 5, 6, 7]]


def _dsize(dt):
    return {F32: 4, BF16: 2, I32: 4}[dt]


class Res:
    __slots__ = ("w", "r", "sem", "val", "name", "dedicated", "ps")

    def __init__(self, name="", dedicated=False):
        self.w = {}
        self.r = {}
        self.sem = None
        self.val = 0
        self.name = name
        self.dedicated = dedicated
        self.ps = None


class Q:
    def __init__(self, kb, name):
        self.kb = kb
        self.name = name
        self.ops = []
        self.n = 0
        self.sem = kb.new_sem(name)
        self.seen = {}

    def op(self, fn, reads=(), writes=(), after=(), dma=None, inc=16):
        waits = {}

        def need(tok):
            if tok is None:
                return
            sem, val = tok
            k = id(sem)
            if self.seen.get(k, 0) >= val:
                return
            if k not in waits or waits[k][1] < val:
                waits[k] = (sem, val)

        for r in reads:
            for t in r.w.values():
                need(t)
        for r in writes:
            for t in r.w.values():
                if dma is not None and r is dma and r.sem is not None and t[0] is r.sem:
                    continue
                need(t)
            for t in r.r.values():
                need(t)
        for t in after:
            need(t)
        if dma is not None and dma.sem is None:
            prev = self.kb.assign_dma_sem(dma)
            need(prev)
        if id(self) in getattr(self.kb, 'fence_pending', ()):
            self.kb.fence_pending.discard(id(self))
            for t in self.kb.fence:
                if t[0] is not self.sem:
                    need(t)
        for k, (sem, val) in waits.items():
            self.seen[k] = val
        if dma is None:
            self.n += 1
            tok = (self.sem, self.n)
            incsem, incamt = self.sem, 1
        else:
            dma.val += inc
            tok = (dma.sem, dma.val)
            if dma.ps is not None:
                dma.ps[1] = dma.val
            incsem, incamt = dma.sem, inc
        self.ops.append((list(waits.values()), fn, incsem, incamt))
        for r in reads:
            k = id(tok[0])
            if k not in r.r or r.r[k][1] < tok[1]:
                r.r[k] = tok
        for r in writes:
            r.w[id(tok[0])] = tok
            r.r = {}
        return tok

    def replay(self, e):
        for waits, fn, incsem, incamt in self.ops:
            for sem, val in waits:
                e.wait_ge(sem, val)
            ins = fn(e)
            ins.then_inc(incsem, incamt)


class KB:
    def __init__(self, arena_bytes=207 * 1024):
        self.nc = bass.Bass("TRN2", target_bir_lowering=False)
        self.stack = contextlib.ExitStack()
        self.nsem = 0
        self.PE = Q(self, "pe")
        self.ACT = Q(self, "act")
        self.DVE = Q(self, "dve")
        self.POOL = Q(self, "pool")
        self.SP = Q(self, "sp")
        self.queues = [self.PE, self.ACT, self.DVE, self.POOL, self.SP]
        self.arena_bytes = arena_bytes
        self.arena = self.stack.enter_context(self.nc.sbuf_tensor("arena", [128, arena_bytes // 2], BF16))
        self.top = 0
        self.banks = []
        self.bank_res = []
        for i in range(8):
            t = self.stack.enter_context(self.nc.psum_tensor("psb%d" % i, [128, 512], F32))
            self.banks.append(t[:, :])
            self.bank_res.append(Res("bank%d" % i))
        self.dma_res = []
        self.tracked = []
        self.pool = []
        self.pool_i = 0
        self.fence = []
        self.fence_pending = set()

    def new_sem(self, name):
        self.nsem += 1
        return self.stack.enter_context(self.nc.semaphore("s%d_%s" % (self.nsem, name)))

    POOLN = 44

    def assign_dma_sem(self, res):
        if res.dedicated:
            res.sem = self.new_sem("d_" + res.name)
            return None
        if len(self.pool) < self.POOLN:
            ps = [self.new_sem("pool%d" % len(self.pool)), 0]
            self.pool.append(ps)
        else:
            ps = self.pool[self.pool_i % self.POOLN]
            self.pool_i += 1
        res.ps = ps
        res.sem = ps[0]
        res.val = ps[1]
        return (ps[0], ps[1]) if ps[1] > 0 else None

    def alloc(self, shape, dt, parts=128, name=""):
        n = 1
        for s in shape:
            n *= s
        nbytes = (n * _dsize(dt) + 63) // 64 * 64
        off = self.top
        self.top += nbytes
        assert self.top <= self.arena_bytes, "SBUF arena overflow %d (%s)" % (self.top, name)
        ap = self.arena[0:parts, off // 2: off // 2 + n * _dsize(dt) // 2]
        if dt != BF16:
            ap = ap.bitcast(dt)
        if len(shape) == 2:
            ap = ap.rearrange("p (a b) -> p a b", a=shape[0])
        elif len(shape) == 3:
            ap = ap.rearrange("p (a b c) -> p a b c", a=shape[0], b=shape[1])
        return ap

    def mark(self):
        return self.top

    def release(self, m):
        if m < self.top:
            toks = [(q.sem, q.n) for q in self.queues if q.n > 0]
            for r in self.tracked:
                if r.sem is not None and r.val > 0:
                    toks.append((r.sem, r.val))
            self.fence = toks
            self.fence_pending = set(id(q) for q in self.queues)
        self.top = m

    def dram(self, name, shape, dt, kind=None):
        if kind is None:
            t = self.nc.dram_tensor(name, shape, dt)
        else:
            t = self.nc.dram_tensor(name, shape, dt, kind=kind)
        return t.ap()

    def join(self, tracked=None):
        toks = [(q.sem, q.n) for q in self.queues if q.n > 0]
        for r in self.tracked:
            if r.sem is not None and r.val > 0:
                toks.append((r.sem, r.val))
        self.fence = toks
        self.fence_pending = set(id(q) for q in self.queues)

    def final_sync(self):
        toks = [(q.sem, q.n) for q in self.queues if q.n > 0 and q is not self.SP]
        for r in self.tracked:
            if r.sem is not None and r.val > 0:
                toks.append((r.sem, r.val))
        self.SP.op(lambda e: e.nop(), after=toks)

    def finish(self):
        with self.nc.Block() as block:
            @block.tensor
            def _(e):
                self.PE.replay(e)

            @block.scalar
            def _(e):
                self.ACT.replay(e)

            @block.vector
            def _(e):
                self.DVE.replay(e)

            @block.gpsimd
            def _(e):
                self.POOL.replay(e)

            @block.sync
            def _(e):
                self.SP.replay(e)
        self.stack.close()
        return self.nc


def mm(kb, out_ap, pairs, reads, writes, first=True, last=True):
    pairs = list(pairs)

    def fn(e):
        ins = None
        n = len(pairs)
        for i, (l, r) in enumerate(pairs):
            ins = e.matmul(out_ap, lhsT=l, rhs=r, start=(first and i == 0), stop=(last and i == n - 1))
        return ins
    return kb.PE.op(fn, reads=reads, writes=writes)


def mm_multi(kb, items, reads, writes):
    items = list(items)

    def fn(e):
        ins = None
        for (o, l, r, st, sp) in items:
            ins = e.matmul(o, lhsT=l, rhs=r, start=st, stop=sp)
        return ins
    return kb.PE.op(fn, reads=reads, writes=writes)


def act(kb, out, in_, func, reads, writes, **kw):
    return kb.ACT.op(lambda e: e.activation(out=out, in_=in_, func=func, **kw), reads=reads, writes=writes)


def vec(q, name, reads, writes, **kw):
    return q.op(lambda e: getattr(e, name)(**kw), reads=reads, writes=writes)


def dma(q, out, in_, reads, writes, res):
    return q.op(lambda e: e.dma_start(out=out, in_=in_), reads=reads, writes=writes, dma=res)


class Ring:
    def __init__(self, kb, shape, dt, n, name, parts=128):
        self.bufs = [(kb.alloc(shape, dt, parts=parts, name=name), Res("%s%d" % (name, i))) for i in range(n)]
        self.i = 0

    def next(self):
        b = self.bufs[self.i % len(self.bufs)]
        self.i += 1
        return b


class Banks:
    def __init__(self, kb, ids):
        self.l = [(kb.banks[i], kb.bank_res[i]) for i in ids]
        self.i = 0

    def next(self):
        b = self.l[self.i % len(self.l)]
        self.i += 1
        return b


CB_ONES, CB_ID, CB_UT = 0, 128, 256
CB_MNC = 384
CB_MNS = CB_MNC + 2048
CB_MMS = CB_MNS + 2048
NCB = CB_MMS + 2048


def host_consts():
    c = np.zeros((128, NCB), np.float32)
    p = np.arange(128)[:, None]
    q = np.arange(128)[None, :]
    c[:, CB_ONES:CB_ONES + 128] = 1.0
    c[:, CB_ID:CB_ID + 128] = (p == q)
    c[:, CB_UT:CB_UT + 128] = (p >= q)
    for i in range(4):
        for j in range(4):
            if j < i:
                mc = np.full((128, 128), NEG, np.float32)
                ms = mc.copy()
            elif j == i:
                mc = np.where(p <= q, 0.0, NEG).astype(np.float32)
                ms = np.where(p < q, 0.0, NEG).astype(np.float32)
            else:
                mc = np.zeros((128, 128), np.float32)
                ms = mc.copy()
            c[:, CB_MNC + i * 512 + j * 128: CB_MNC + i * 512 + (j + 1) * 128] = mc
            c[:, CB_MNS + i * 512 + j * 128: CB_MNS + i * 512 + (j + 1) * 128] = ms
            c[:, CB_MMS + i * 512 + j * 128: CB_MMS + i * 512 + (j + 1) * 128] = (ms == 0.0)
    return c


class St:
    pass


def emit_rmsnorm(kb, S, src_fn, rsrc, nk, n, gain_ap, dst_fn, rdst, sq, rsq, rstd, rrstd, bank, rbank, dfeat):
    for c in range(nk):
        act(kb, sq[:, c, 0:n], src_fn(c), AF.Square, reads=[rsrc], writes=[rsq])
    mm(kb, bank[:, 0:n], [(S.ones, sq[:, c, 0:n]) for c in range(nk)], reads=[S.rcb, rsq], writes=[rbank])
    act(kb, rstd[:, 0:n], bank[:, 0:n], AF.Ln, reads=[rbank, S.reps], writes=[rrstd], bias=S.eps, scale=1.0 / dfeat)
    act(kb, rstd[:, 0:n], rstd[:, 0:n], AF.Exp, reads=[rrstd], writes=[rrstd], scale=-0.5)
    for c in range(nk):
        vec(kb.DVE, "scalar_tensor_tensor", reads=[rsrc, rrstd, S.rgain], writes=[rdst],
            out=dst_fn(c), in0=src_fn(c), scalar=gain_ap[:, c:c + 1], in1=rstd[:, 0:n], op0=ALU.mult, op1=ALU.mult)


KINDS = ["sb", "fox", "swa", "mla"]
WIN_COLS = {"sb": 1664, "fox": 1668, "swa": 832, "mla": 1024}


def declare_io(kb, S, layers, stop=99):
    I = {}

    def inp(name, shape, dt=F32):
        I[name] = kb.dram(name, shape, dt, "ExternalInput")
    inp("xT", [D, TOK])
    inp("memT", [D, 256])
    inp("pos", [1, SEQ], I32)
    inp("cst", [128, NCB])
    inp("gains", [128, 4 * 3 * 16 + 16])
    inp("idx_o", [128, 4], I32)
    inp("idx_h", [128, 16], I32)
    inp("misc", [128, 8])
    for l in layers:
        if stop <= 1:
            break
        kind = KINDS[l % 4]
        inp("win%d" % l, [128, 16, WIN_COLS[kind]])
        inp("wmkv%d" % l, [128, 16, 256])
        if stop > 3:
            inp("wout%d" % l, [16, 128, 20, 128])
            inp("wup%d" % l, [NFF, 128, 16, 256])
            inp("convw%d" % l, [128, NFF * 2 * 3])
            inp("convb%d" % l, [128, NFF * 2])
            inp("wdown%d" % l, [2, 16, 128, 22, 128])
        if kind == "fox":
            inp("bf%d" % l, [4, 1])
        if kind == "swa":
            inp("swab%d" % l, [2, 128, 8 * 128])
            inp("sinks%d" % l, [1, 8])
            inp("relb%d" % l, [1, 8 * 32])
        if kind == "mla":
            inp("wuq%d" % l, [128, 4, 1024])
            inp("wukv%d" % l, [128, 2, 1024])
            inp("lnorm%d" % l, [128, 6])
    S.I = I
    S.outT = kb.dram("outT", [D, TOK], F32, "ExternalOutput")
    S.r_out = Res("outT", True)
    S.hT_loc = [kb.dram("hT_loc%d" % q, [512, TOK], BF16) for q in range(4)]
    S.hT_all = [kb.dram("hT_all%d" % q, [4 * 512, TOK], BF16) for q in range(4)]
    S.o_loc = [kb.dram("o_loc%d" % i, [128, SEQ], BF16) for i in range(5)]
    S.o_all = [kb.dram("o_all%d" % i, [4 * 128, SEQ], BF16) for i in range(5)]
    S.halo_loc = kb.dram("halo_loc", [D, 2], BF16)
    S.halo_all = kb.dram("halo_all", [4 * D, 2], BF16)
    S.xT_scr = kb.dram("xT_scr", [D, TOK], F32)
    for nm in ["halo_loc", "halo_all", "xT_scr"]:
        setattr(S, "r_" + nm, Res(nm, True))
    S.r_hT_loc = [Res("hT_loc%d" % q, True) for q in range(4)]
    S.r_hT_all = [Res("hT_all%d" % q, True) for q in range(4)]
    S.r_o_loc = [Res("o_loc%d" % i, True) for i in range(5)]
    S.r_o_all = [Res("o_all%d" % i, True) for i in range(5)]
    S.rX = [Res("xT0", True), Res("xT1", True)]


def emit_setup(kb, S):
    I = S.I
    S.cb = kb.alloc([NCB], BF16, name="cb")
    S.rcb = Res("cb", True)
    dma(kb.POOL, S.cb, I["cst"], [], [S.rcb], S.rcb)
    S.ones = S.cb[:, CB_ONES:CB_ONES + 128]
    S.ident = S.cb[:, CB_ID:CB_ID + 128]
    S.ut = S.cb[:, CB_UT:CB_UT + 128]
    S.gains = kb.alloc([4 * 3 * 16 + 16], F32, name="gains")
    S.rgain = Res("gains", True)
    dma(kb.SP, S.gains, I["gains"], [], [S.rgain], S.rgain)
    S.idx_o = kb.alloc([4], I32)
    S.idx_h = kb.alloc([16], I32)
    S.misc = kb.alloc([8], F32)
    S.ridx = Res("idx", True)
    dma(kb.SP, S.idx_o, I["idx_o"], [], [S.ridx], S.ridx)
    dma(kb.SP, S.idx_h, I["idx_h"], [], [S.ridx], S.ridx)
    dma(kb.SP, S.misc, I["misc"], [], [S.ridx], S.ridx)
    S.eps = kb.alloc([1], F32)
    S.reps = Res("eps")
    vec(kb.DVE, "memset", [], [S.reps], ap=S.eps, constant=EPS)
    S.base_top = kb.mark()
    kb.tracked += [S.rcb, S.rgain, S.ridx, S.r_out, S.r_halo_loc, S.r_halo_all, S.r_xT_scr] + S.r_hT_loc + S.r_hT_all \
        + S.r_o_loc + S.r_o_all
    S.tracked = kb.tracked
    S.tracked += S.rX


def gain_ap(S, l, which):
    if l < 0:
        off = 4 * 3 * 16
    else:
        off = (l * 3 + which) * 16
    return S.gains[:, off:off + 16]


def alloc_x(kb, S):
    kb.release(S.base_top)
    S.xT = kb.alloc([NCH, TOK], F32, name="xT")
    if os.environ.get("GUARD"):
        kb.alloc([int(os.environ["GUARD"])], F32, name="guard")


def emit_load_x(kb, S, src):
    v = src.rearrange("(c p) t -> p c t", p=128)
    for t in range(2):
        dma(kb.SP, S.xT[:, :, t * 512:(t + 1) * 512], v[:, :, t * 512:(t + 1) * 512], [S.r_xT_scr], [S.rX[t]], S.rX[t])


def emit_norm_out(kb, S, l, final):
    m0 = kb.mark()
    sq = kb.alloc([NCH, 512], BF16, name="nsq")
    rsq = Res("nsq")
    rstd = kb.alloc([512], F32)
    rrstd = Res("nrstd")
    odt = F32 if final else BF16
    stg = Ring(kb, [NCH, 512], odt, 1 if final else 2, "nstg")
    g = gain_ap(S, -1 if final else l, 0)
    if final:
        dv = S.outT.rearrange("(c p) t -> p c t", p=128)
    for t in range(2):
        bank, rbank = kb.banks[t], kb.bank_res[t]
        so, rso = stg.next()
        emit_rmsnorm(kb, S, lambda c, t=t: S.xT[:, c, t * 512:(t + 1) * 512], S.rX[t], NCH, 512, g,
                     lambda c, so=so: so[:, c, :], rso, sq, rsq, rstd, rrstd, bank, rbank, D)
        S.tracked.append(rso)
        if final:
            dma(kb.SP, dv[:, :, t * 512:(t + 1) * 512], so, [rso], [S.r_out], S.r_out)
        else:
            for q in range(4):
                dq = S.hT_loc[q].rearrange("(c p) t -> p c t", p=128)
                dma(kb.SP, dq[:, :, t * 512:(t + 1) * 512], so[:, 4 * q:4 * q + 4, :], [rso], [S.r_hT_loc[q]], S.r_hT_loc[q])
    kb.release(m0)


def emit_spill_x(kb, S):
    v = S.xT_scr.rearrange("(c p) t -> p c t", p=128)
    for t in range(2):
        dma(kb.SP, v[:, :, t * 512:(t + 1) * 512], S.xT[:, :, t * 512:(t + 1) * 512], [S.rX[t]], [S.r_xT_scr], S.r_xT_scr)


def emit_allgather(kb, S, src, rsrc, dst, rdst):
    def fn(e):
        return e.collective_compute("AllGather", ALU.bypass, replica_groups=GROUPS,
                                    ins=[src.opt()], outs=[dst.opt()])
    kb.POOL.op(fn, reads=[rsrc], writes=[rdst], dma=rdst, inc=1)


def emit_outproj(kb, S, l):
    I = S.I
    m0 = kb.mark()
    otok = kb.alloc([20, TOK], BF16, name="otok")
    rot = [Res("otok%d" % i) for i in range(4)]
    S.tracked += rot
    for k in range(20):
        r_, i_ = k // 5, k % 5
        gv = S.o_all[i_].rearrange("r (k c) -> (r k) c", k=4)

        def fn(e, k=k, gv=gv, r_=r_):
            return e.indirect_dma_start(out=otok[:, k, :], out_offset=None, in_=gv,
                                        in_offset=bass.IndirectOffsetOnAxis(ap=S.idx_o[:, r_:r_ + 1], axis=0))
        kb.POOL.op(fn, reads=[S.r_o_all[i_], S.ridx], writes=[rot[k // 5]], dma=rot[k // 5])
    wr = Ring(kb, [20, 128], BF16, 2, "wout")
    S.tracked += [r for _, r in wr.bufs]
    bk = Banks(kb, [0, 1, 2, 3])
    morder = list(range(16))
    if os.environ.get("REVM"):
        morder = morder[::-1]
    nxt = wr.next()
    dma(kb.POOL, nxt[0], I["wout%d" % l][morder[0]], [], [nxt[1]], nxt[1])
    for mi, m in enumerate(morder):
        w, rw = nxt
        if mi + 1 < 16:
            nxt = wr.next()
            dma(kb.POOL, nxt[0], I["wout%d" % l][morder[mi + 1]], [], [nxt[1]], nxt[1])
        for t in range(2):
            bank, rbank = bk.next()
            mm(kb, bank, [(w[:, k, :], otok[:, k, t * 512:(t + 1) * 512]) for k in range(20)],
               reads=[rw] + rot, writes=[rbank])
            xs = S.xT[:, m, t * 512:(t + 1) * 512]
            vec(kb.DVE, "tensor_tensor", reads=[rbank], writes=[S.rX[t]], out=xs, in0=xs, in1=bank, op=ALU.add)
    kb.release(m0)


FFN_TILES = [(0, 512, 0, 510), (510, 512, 510, 510), (1020, 6, 1020, 4)]


def emit_ffn(kb, S, l):
    I = S.I
    PE, ACT, DVE, POOL, SP = kb.PE, kb.ACT, kb.DVE, kb.POOL, kb.SP
    m0 = kb.mark()
    h2 = kb.alloc([NCH, TOK + 2], BF16, name="h2")
    rh2 = [Res("h2a"), Res("h2b"), Res("h2halo")]
    m1 = kb.mark()
    sq = kb.alloc([NCH, 512], BF16)
    rsq = Res("sq2")
    rstd = kb.alloc([512], F32)
    rrstd = Res("rstd2")
    g = gain_ap(S, l, 1)
    for t in range(2):
        bank, rbank = kb.banks[6 + t], kb.bank_res[6 + t]
        emit_rmsnorm(kb, S, lambda c, t=t: S.xT[:, c, t * 512:(t + 1) * 512], S.rX[t], NCH, 512, g,
                     lambda c, t=t: h2[:, c, 2 + t * 512: 2 + (t + 1) * 512], rh2[t], sq, rsq, rstd, rrstd, bank, rbank, D)
    kb.release(m1)
    hv = S.halo_loc.rearrange("(c p) t -> p c t", p=128)
    dma(SP, hv, h2[:, :, TOK:TOK + 2], [rh2[1]], [S.r_halo_loc], S.r_halo_loc)
    kb.join(S.tracked)
    emit_allgather(kb, S, S.halo_loc, S.r_halo_loc, S.halo_all, S.r_halo_all)
    kb.join(S.tracked)
    hraw = kb.alloc([NCH, 2], BF16)
    rhraw = Res("hraw")
    S.tracked.append(rhraw)
    for c in range(NCH):
        def fn(e, c=c):
            return e.indirect_dma_start(out=hraw[:, c, :], out_offset=None, in_=S.halo_all,
                                        in_offset=bass.IndirectOffsetOnAxis(ap=S.idx_h[:, c:c + 1], axis=0))
        POOL.op(fn, reads=[S.r_halo_all, S.ridx], writes=[rhraw], dma=rhraw)
    vec(DVE, "tensor_scalar", reads=[rhraw, S.ridx], writes=[rh2[2]], out=h2[:, :, 0:2], in0=hraw,
        scalar1=S.misc[:, 0:1], scalar2=None, op0=ALU.mult)
    cw = kb.alloc([NFF * 6], F32)
    cbias = kb.alloc([NFF * 2], F32)
    rcw = Res("convw")
    S.tracked.append(rcw)
    dma(SP, cw, I["convw%d" % l], [], [rcw], rcw)
    dma(SP, cbias, I["convb%d" % l], [], [rcw], rcw)
    gT = kb.alloc([22, TOK], BF16, name="gT")
    rg = [Res("g%d" % i) for i in range(22)]
    wur = Ring(kb, [NCH, 256], BF16, 2, "wup")
    wdr = Ring(kb, [22, 128], BF16, 2, "wdn")
    S.tracked += [r for _, r in wur.bufs] + [r for _, r in wdr.bufs]
    ar = Ring(kb, [512], F32, 4, "cva")
    bk = Banks(kb, [0, 1, 2, 3])
    bkd = Banks(kb, [4, 5])
    nxt = wur.next()
    dma(POOL, nxt[0], I["wup%d" % l][0], [], [nxt[1]], nxt[1])
    for f in range(NFF):
        w, rw = nxt
        if f + 1 < NFF:
            nxt = wur.next()
            dma(POOL, nxt[0], I["wup%d" % l][f + 1], [], [nxt[1]], nxt[1])
        fl = f % 22
        for (c0, N, t0, n) in FFN_TILES:
            rdeps = [rh2[0], rh2[1], rh2[2]]
            a = []
            for gvi in range(2):
                bank, rbank = bk.next()
                mm(kb, bank[:, 0:N], [(w[:, c, gvi * 128:(gvi + 1) * 128], h2[:, c, c0:c0 + N]) for c in range(NCH)],
                   reads=[rw] + rdeps, writes=[rbank])
                at, rat = ar.next()
                wi = (f * 2 + gvi) * 3
                act(kb, at[:, 0:n], bank[:, 0:n], AF.Identity, reads=[rbank, rcw], writes=[rat],
                    bias=cbias[:, f * 2 + gvi: f * 2 + gvi + 1], scale=cw[:, wi:wi + 1])
                vec(DVE, "scalar_tensor_tensor", reads=[rbank, rat, rcw], writes=[rat], out=at[:, 0:n],
                    in0=bank[:, 1:n + 1], scalar=cw[:, wi + 1:wi + 2], in1=at[:, 0:n], op0=ALU.mult, op1=ALU.add)
                vec(DVE, "scalar_tensor_tensor", reads=[rbank, rat, rcw], writes=[rat], out=at[:, 0:n],
                    in0=bank[:, 2:n + 2], scalar=cw[:, wi + 2:wi + 3], in1=at[:, 0:n], op0=ALU.mult, op1=ALU.add)
                a.append((at, rat))
            (ag, rag), (av, rav) = a
            act(kb, ag[:, 0:n], ag[:, 0:n], AF.Silu, reads=[rag], writes=[rag])
            vec(DVE, "tensor_tensor", reads=[rag, rav], writes=[rg[fl]], out=gT[:, fl, t0:t0 + n],
                in0=ag[:, 0:n], in1=av[:, 0:n], op=ALU.mult)
        if fl == 21 and not getattr(S, 'no_down', False):
            hf = f // 22
            nd = wdr.next()
            dma(POOL, nd[0], I["wdown%d" % l][hf, 0], [], [nd[1]], nd[1])
            for m in range(16):
                wd, rwd = nd
                if m + 1 < 16:
                    nd = wdr.next()
                    dma(POOL, nd[0], I["wdown%d" % l][hf, m + 1], [], [nd[1]], nd[1])
                for t in range(2):
                    bank, rbank = bkd.next()
                    mm(kb, bank, [(wd[:, k, :], gT[:, k, t * 512:(t + 1) * 512]) for k in range(22)],
                       reads=[rwd] + rg, writes=[rbank])
                    xs = S.xT[:, m, t * 512:(t + 1) * 512]
                    vec(DVE, "tensor_tensor", reads=[rbank], writes=[S.rX[t]], out=xs, in0=xs, in1=bank, op=ALU.add)
    kb.release(m0)


def emit_memkv(kb, S, l):
    I = S.I
    KmT = kb.alloc([256], BF16, name="KmT")
    Vm = kb.alloc([2, 128], BF16, name="Vm")
    rkm = Res("memkv")
    m0 = kb.mark()
    memT = kb.alloc([NCH, 256], F32)
    rmem = Res("memT")
    memh = kb.alloc([NCH, 256], BF16)
    rmh = Res("memh")
    sq = kb.alloc([NCH, 256], BF16)
    rsq = Res("msq")
    rstd = kb.alloc([256], F32)
    rrstd = Res("mrstd")
    wm = kb.alloc([NCH, 256], BF16)
    rwm = Res("wmkv")
    S.tracked += [rmem, rwm]
    dma(kb.SP, memT, S.I["memT"].rearrange("(c p) m -> p c m", p=128), [], [rmem], rmem)
    dma(kb.POOL, wm, I["wmkv%d" % l], [], [rwm], rwm)
    bank, rbank = kb.banks[6], kb.bank_res[6]
    emit_rmsnorm(kb, S, lambda c: memT[:, c, :], rmem, NCH, 256, gain_ap(S, l, 2), lambda c: memh[:, c, :], rmh,
                 sq, rsq, rstd, rrstd, bank, rbank, D)
    b2, rb2 = kb.banks[7], kb.bank_res[7]
    mm(kb, b2[:, 0:256], [(wm[:, c, 0:128], memh[:, c, :]) for c in range(NCH)], reads=[rwm, rmh], writes=[rb2])
    act(kb, KmT, b2[:, 0:256], AF.Copy, reads=[rb2], writes=[rkm])
    for mb in range(2):
        mm(kb, bank[:, 0:128], [(memh[:, c, mb * 128:(mb + 1) * 128], wm[:, c, 128:256]) for c in range(NCH)],
           reads=[rwm, rmh], writes=[rbank])
        act(kb, Vm[:, mb, :], bank[:, 0:128], AF.Copy, reads=[rbank], writes=[rkm])
    kb.release(m0)
    return KmT, Vm, rkm


def emit_bound(kb, S, qparts, rq, T, kparts, rk, Sk, negM, rnegM):
    m0 = kb.mark()
    sqr = Ring(kb, [512], BF16, 2, "bsq")
    mx = kb.alloc([4], F32)
    rmx = Res("bmx")
    bk = Banks(kb, [6, 7])
    for side, (parts, rr, n) in enumerate([(qparts, rq, T), (kparts, rk, Sk)]):
        nt = (n + 511) // 512
        for t in range(nt):
            w = min(512, n - t * 512)
            bank, rbank = bk.next()
            pairs = []
            for pi, ap in enumerate(parts):
                kd = ap.shape[0]
                sq, rsq = sqr.next()
                act(kb, sq[0:kd, 0:w], ap[:, t * 512:t * 512 + w], AF.Square, reads=rr, writes=[rsq])
                mm(kb, bank[:, 0:w], [(S.ones[0:kd, :], sq[0:kd, 0:w])], reads=[S.rcb, rsq], writes=[rbank],
                   first=(pi == 0), last=(pi == len(parts) - 1))
            col = mx[:, side * 2: side * 2 + 1]
            if t == 0:
                vec(kb.DVE, "reduce_max", reads=[rbank], writes=[rmx], out=col, in_=bank[:, 0:w], axis=AX.X)
            else:
                tmp = mx[:, side * 2 + 1: side * 2 + 2]
                vec(kb.DVE, "reduce_max", reads=[rbank], writes=[rmx], out=tmp, in_=bank[:, 0:w], axis=AX.X)
                vec(kb.DVE, "tensor_tensor", reads=[rmx], writes=[rmx], out=col, in0=col, in1=tmp, op=ALU.max)
    vec(kb.DVE, "tensor_tensor", reads=[rmx], writes=[rmx], out=mx[:, 0:1], in0=mx[:, 0:1], in1=mx[:, 2:3], op=ALU.mult)
    act(kb, mx[:, 0:1], mx[:, 0:1], AF.Ln, reads=[rmx], writes=[rmx])
    act(kb, mx[:, 0:1], mx[:, 0:1], AF.Exp, reads=[rmx], writes=[rmx], scale=0.5)
    vec(kb.DVE, "tensor_scalar", reads=[rmx], writes=[rnegM], out=negM, in0=mx[:, 0:1], scalar1=-1.0, scalar2=None,
        op0=ALU.mult)
    kb.release(m0)


def emit_softmax_attn(kb, S, qk_parts, rq_fn, rk_fn, V_fn, T, nkb_fn, mask_fn, negM, rnegM, row, extra=None):
    m0 = kb.mark()
    pr = Ring(kb, [512], BF16, 3, "P")
    rinv_r = Ring(kb, [512], F32, 2, "rinv")
    ostg = Ring(kb, [512], BF16, 2, "ostg")
    S.tracked += [r for _, r in ostg.bufs]
    zb = Banks(kb, [0, 1])
    ob = Banks(kb, [2, 3])
    rb = Banks(kb, [4, 5])
    blocks = []
    for qc in range(T // 512):
        nkb = nkb_fn(qc)
        for kbk in range(nkb):
            blocks.append((qc, kbk, kbk == 0, kbk == nkb - 1))
    st = {}

    def stage_qk(i):
        qc, kbk, first, last = blocks[i]
        Z, rZ = zb.next()
        qs = slice(qc * 512, (qc + 1) * 512)
        ks = slice(kbk * 128, (kbk + 1) * 128)
        pairs = [(KT[:, ks], QT[:, qs]) for (QT, KT) in qk_parts]
        reads = list(rq_fn(qc)) + list(rk_fn(kbk))
        if extra is not None:
            QA, KA, rqa = extra
            pairs.append((KA[:, ks], QA[:, qs]))
            reads.append(rqa)
        mk = mask_fn(qc, kbk)
        if mk is not None:
            pairs.append((S.ident, mk))
            reads.append(S.rcb)
        mm(kb, Z, pairs, reads=reads, writes=[rZ])
        st[i] = (Z, rZ)

    def stage_av(i):
        qc, kbk, first, last = blocks[i]
        Z, rZ = st.pop(i)
        if first:
            st["O"] = ob.next()
            st["R"] = rb.next()
        O, rO = st["O"]
        R, rR = st["R"]
        P, rP = pr.next()
        act(kb, P, Z, AF.Exp, reads=[rZ, rnegM], writes=[rP], bias=negM, scale=1.0)
        mm(kb, O, [(V_fn(kbk), P)], reads=[rP] + list(rk_fn(kbk)), writes=[rO], first=first, last=last)
        mm(kb, R, [(S.ones, P)], reads=[rP, S.rcb], writes=[rR], first=first, last=last)
        if last:
            qs = slice(qc * 512, (qc + 1) * 512)
            rinv, rri = rinv_r.next()
            vec(kb.DVE, "reciprocal", reads=[rR], writes=[rri], out=rinv, in_=R)
            og, rog = ostg.next()
            vec(kb.DVE, "tensor_tensor", reads=[rO, rri], writes=[rog], out=og, in0=O, in1=rinv, op=ALU.mult)
            dma(kb.SP, S.o_loc[row][:, qs], og, [rog], [S.r_o_loc[row]], S.r_o_loc[row])

    n = len(blocks)
    stage_qk(0)
    for i in range(n):
        if i + 1 < n:
            stage_qk(i + 1)
        stage_av(i)
    kb.release(m0)


def emit_sb_attn(kb, S, QT, KT, V_fn, rq_fn, rk_fn, T, row):
    m0 = kb.mark()
    er = Ring(kb, [512], F32, 2, "sbe")
    lr = Ring(kb, [512], BF16, 3, "sbl")
    ar = Ring(kb, [512], BF16, 3, "sba")
    chr_ = Ring(kb, [512], BF16, 2, "sbch", parts=1)
    clr = Ring(kb, [512], BF16, 2, "sbcl", parts=1)
    ostg = Ring(kb, [512], BF16, 2, "sbo")
    S.tracked += [r for _, r in ostg.bufs]
    zb = Banks(kb, [0, 1, 6])
    ob = Banks(kb, [2, 3])
    cb = Banks(kb, [4, 5])
    blocks = []
    for qc in range(T // 512):
        order = list(range(4 * qc + 3, -1, -1))
        for n_i, kbk in enumerate(order):
            blocks.append((qc, kbk, n_i == 0, n_i == len(order) - 1))
    st = {}

    def stage1(i):
        qc, kbk, first, last = blocks[i]
        Z, rZ = zb.next()
        qs = slice(qc * 512, (qc + 1) * 512)
        ks = slice(kbk * 128, (kbk + 1) * 128)
        mm(kb, Z, [(KT[:, ks], QT[:, qs])], reads=list(rq_fn(qc)) + list(rk_fn(kbk)), writes=[rZ], first=True, last=True)
        e_, re_ = er.next()
        act(kb, e_, Z, AF.Exp, reads=[rZ], writes=[re_])
        act(kb, e_, e_, AF.Ln, reads=[re_], writes=[re_], bias=1.0)
        L, rL = lr.next()
        di = kbk - 4 * qc
        if di >= 0:
            mmul = S.cb[:, CB_MMS + di * 512: CB_MMS + (di + 1) * 512]
            vec(kb.DVE, "scalar_tensor_tensor", reads=[re_, S.rcb], writes=[rL], out=L, in0=e_, scalar=-1.0, in1=mmul,
                op0=ALU.mult, op1=ALU.mult)
        else:
            vec(kb.DVE, "tensor_scalar", reads=[re_], writes=[rL], out=L, in0=e_, scalar1=-1.0, scalar2=None, op0=ALU.mult)
        st[i] = (Z, rZ, L, rL)

    def stage2(i):
        qc, kbk, first, last = blocks[i]
        Z, rZ, L, rL = st.pop(i)
        if first:
            st["O"] = ob.next()
            st["C"] = cb.next()
        C, rC = st["C"]
        di = kbk - 4 * qc
        pairs = [(S.ut, L)]
        reads = [rL, S.rcb]
        if not first:
            ch, rch = chr_.next()
            cl, rcl = clr.next()
            vec(kb.DVE, "tensor_copy", reads=[rC], writes=[rch], out=ch, in_=C[0:1, :])
            vec(kb.DVE, "tensor_tensor", reads=[rC, rch], writes=[rcl], out=cl, in0=C[0:1, :], in1=ch, op=ALU.subtract)
            pairs += [(S.ones[0:1, :], ch), (S.ones[0:1, :], cl)]
            reads += [rch, rcl]
        if di >= 0:
            pairs.append((S.ident, S.cb[:, CB_MNS + di * 512: CB_MNS + (di + 1) * 512]))
        mm(kb, Z, pairs, reads=reads, writes=[rZ], first=False, last=True)
        if not last:
            mm(kb, C[0:1, :], [(S.ones[:, 0:1], L)], reads=[rL, S.rcb], writes=[rC], first=first, last=True)
        A, rA = ar.next()
        act(kb, A, Z, AF.Exp, reads=[rZ], writes=[rA])
        st[("A", i)] = (A, rA)

    def stage3(i):
        qc, kbk, first, last = blocks[i]
        A, rA = st.pop(("A", i))
        O, rO = st["O"] if not first else st["O"]
        mm(kb, O, [(V_fn(kbk), A)], reads=[rA] + list(rk_fn(kbk)), writes=[rO], first=first, last=last)
        if last:
            qs = slice(qc * 512, (qc + 1) * 512)
            og, rog = ostg.next()
            act(kb, og, O, AF.Copy, reads=[rO], writes=[rog])
            dma(kb.SP, S.o_loc[row][:, qs], og, [rog], [S.r_o_loc[row]], S.r_o_loc[row])

    n = len(blocks)
    stage1(0)
    for i in range(n):
        if i + 1 < n:
            stage1(i + 1)
        stage2(i)
        stage3(i)
    kb.release(m0)


def hT_tile_src(S, tt, q):
    r, half = tt // 2, tt % 2
    v = S.hT_all[q][r * 512:(r + 1) * 512, half * 512:(half + 1) * 512]
    return v.rearrange("(c p) t -> p c t", p=128)


def emit_load_hT(kb, S, h, rh, tt):
    for q in range(4):
        dma(kb.SP, h[:, 4 * q:4 * q + 4, :], hT_tile_src(S, tt, q), [S.r_hT_all[q]], [rh], rh)


def emit_ag_hT(kb, S):
    kb.join(S.tracked)
    for q in range(4):
        emit_allgather(kb, S, S.hT_loc[q], S.r_hT_loc[q], S.hT_all[q], S.r_hT_all[q])
    kb.join(S.tracked)


def emit_ag_o(kb, S, i):
    return


def emit_ag_o_all(kb, S):
    kb.join(S.tracked)
    for i in range(5):
        emit_allgather(kb, S, S.o_loc[i], S.r_o_loc[i], S.o_all[i], S.r_o_all[i])


def emit_mem_bound(kb, S, QmT, rq_all, KmT, rkm, negM, rnegM):
    emit_bound(kb, S, [QmT], rq_all, SEQ, [KmT], [rkm], 256, negM, rnegM)


def emit_mem_attn(kb, S, QmT, rq_fn, KmT, Vm, rkm, negM, rnegM):
    emit_softmax_attn(kb, S, [(QmT, KmT)], rq_fn, lambda k: [rkm], lambda k: Vm[:, k, :], SEQ,
                      lambda qc: 2, lambda qc, k: None, negM, rnegM, 4)
    emit_ag_o(kb, S, 4)


def causal_mask_fn(S):
    def f(qc, kbk):
        di = kbk - 4 * qc
        if di < 0:
            return None
        return S.cb[:, CB_MNC + di * 512: CB_MNC + (di + 1) * 512]
    return f


def emit_phase_b_sbfox(kb, S, l, kind):
    I = S.I
    PE, ACT, DVE, POOL, SP = kb.PE, kb.ACT, kb.DVE, kb.POOL, kb.SP
    kb.release(S.base_top)
    KmT, Vm, rkm = emit_memkv(kb, S, l)
    QT = kb.alloc([4, SEQ], BF16, name="QT")
    KT = kb.alloc([4, SEQ], BF16, name="KT")
    V = kb.alloc([32, 512], BF16, name="V")
    QmT = kb.alloc([SEQ], BF16, name="QmT")
    rT = [Res("qkv%d" % t) for t in range(8)]
    fox = (kind == "fox")
    if fox:
        fT = kb.alloc([SEQ], F32, parts=4, name="fT")
        rf = Res("fT")
    ncols = WIN_COLS[kind]
    m0 = kb.mark()
    W = kb.alloc([NCH, ncols], BF16, name="W")
    rW = Res("W")
    S.tracked.append(rW)
    dma(POOL, W, I["win%d" % l], [], [rW], rW)
    hr = Ring(kb, [NCH, 512], BF16, 1, "hT")
    S.tracked += [r for _, r in hr.bufs]
    bk = Banks(kb, [0, 1, 2, 3])
    sc = 128.0 ** -0.5
    flip = 0
    for tt in range(8):
        h, rh = hr.next()
        emit_load_hT(kb, S, h, rh, tt)
        ts = slice(tt * 512, (tt + 1) * 512)
        for oc in range(9):
            bank, rbank = bk.next()
            mm(kb, bank, [(W[:, c, oc * 128:(oc + 1) * 128] if oc < 8 else W[:, c, 1536:1664], h[:, c, :]) for c in range(NCH)],
               reads=[rW, rh], writes=[rbank])
            if oc < 4:
                act(kb, QT[:, oc, ts], bank, AF.Copy, reads=[rbank], writes=[rT[tt]], scale=sc)
            elif oc < 8:
                vec(DVE, "tensor_copy", reads=[rbank], writes=[rT[tt]], out=KT[:, oc - 4, ts], in_=bank)
            else:
                act(kb, QmT[:, ts], bank, AF.Copy, reads=[rbank], writes=[rT[tt]], scale=sc)
        for tb in range(4):
            bank, rbank = bk.next()
            mm(kb, bank, [(h[:, c, tb * 128:(tb + 1) * 128], W[:, c, 1024:1536]) for c in range(NCH)],
               reads=[rW, rh], writes=[rbank])
            if flip % 2 == 0:
                act(kb, V[:, tt * 4 + tb, :], bank, AF.Copy, reads=[rbank], writes=[rT[tt]])
            else:
                vec(DVE, "tensor_copy", reads=[rbank], writes=[rT[tt]], out=V[:, tt * 4 + tb, :], in_=bank)
            flip += 1
        if fox:
            bank, rbank = bk.next()
            mm(kb, bank[0:4, :], [(W[:, c, 1664:1668], h[:, c, :]) for c in range(NCH)], reads=[rW, rh], writes=[rbank])
            vec(DVE, "tensor_copy", reads=[rbank], writes=[rf], out=fT[:, ts], in_=bank[0:4, :])
    kb.release(m0)
    rq_fn = lambda qc: [rT[qc]]
    rk_fn = lambda kbk: [rT[kbk // 4]]
    negMs = kb.alloc([8], F32, name="negMs")
    rnegM = Res("negMs")
    rq_all = [rT[t] for t in range(8)]
    emit_mem_bound(kb, S, QmT, rq_all, KmT, rkm, negMs[:, 4:5], rnegM)
    if not fox:
        kb.join(S.tracked)
    if not fox:
        for i in range(4):
            emit_sb_attn(kb, S, QT[:, i, :], KT[:, i, :], lambda kbk, i=i: V[:, kbk, i * 128:(i + 1) * 128], rq_fn, rk_fn, SEQ, i)
            emit_ag_o(kb, S, i)
    else:
        m1 = kb.mark()
        nb = kb.alloc([1], F32, parts=4)
        rnb = Res("nbf")
        S.tracked.append(rnb)
        dma(SP, nb, I["bf%d" % l], [], [rnb], rnb)
        ca = fT
        rca = rf
        cbuf = kb.alloc([SEQ], F32, parts=4)
        rcbuf = Res("cbuf")
        vec(DVE, "tensor_scalar", reads=[rf, rnb], writes=[rca], out=ca, in0=fT, scalar1=nb[:, 0:1], scalar2=None, op0=ALU.add)
        act(kb, ca, ca, AF.Exp, reads=[rca], writes=[rca], scale=-1.0)
        act(kb, ca, ca, AF.Ln, reads=[rca], writes=[rca], bias=1.0)
        vec(DVE, "tensor_scalar", reads=[rca], writes=[rca], out=ca, in0=ca, scalar1=-1.0, scalar2=None, op0=ALU.mult)
        src, rsrc, dst, rdst = ca, rca, cbuf, rcbuf
        sh = 1
        while sh < SEQ:
            vec(DVE, "tensor_copy", reads=[rsrc], writes=[rdst], out=dst[:, 0:sh], in_=src[:, 0:sh])
            vec(DVE, "tensor_tensor", reads=[rsrc], writes=[rdst], out=dst[:, sh:SEQ], in0=src[:, sh:SEQ], in1=src[:, 0:SEQ - sh], op=ALU.add)
            src, rsrc, dst, rdst = dst, rdst, src, rsrc
            sh *= 2
        cfin, rcfin, tmp, rtmp = src, rsrc, dst, rdst
        parts3 = kb.alloc([3, SEQ], BF16, parts=4)
        rp6 = Res("p3")
        vec(DVE, "tensor_copy", reads=[rcfin], writes=[rp6], out=parts3[:, 0, :], in_=cfin)
        vec(DVE, "tensor_tensor", reads=[rcfin, rp6], writes=[rtmp], out=tmp, in0=cfin, in1=parts3[:, 0, :], op=ALU.subtract)
        vec(DVE, "tensor_copy", reads=[rtmp], writes=[rp6], out=parts3[:, 1, :], in_=tmp)
        vec(DVE, "tensor_tensor", reads=[rtmp, rp6], writes=[rtmp], out=tmp, in0=tmp, in1=parts3[:, 1, :], op=ALU.subtract)
        vec(DVE, "tensor_copy", reads=[rtmp], writes=[rp6], out=parts3[:, 2, :], in_=tmp)
        QA = kb.alloc([SEQ], BF16, parts=6)
        KA = kb.alloc([SEQ], BF16, parts=6)
        rqa = Res("qaka")
        S.tracked.append(rqa)
        for i in range(4):
            emit_bound(kb, S, [QT[:, i, :]], rq_all, SEQ, [KT[:, i, :]], rq_all, SEQ, negMs[:, i:i + 1], rnegM)
        kb.join(S.tracked)
        if os.environ.get("FOXCUT") == "1":
            return
        for i in range(4):
            negM = negMs[:, i:i + 1]
            vec(DVE, "memset", reads=[], writes=[rqa], ap=QA, constant=-1.0)
            vec(DVE, "memset", reads=[], writes=[rqa], ap=KA, constant=1.0)
            for r3 in range(3):
                dma(SP, QA[r3:r3 + 1, :], parts3[i:i + 1, r3, :], [rp6], [rqa], rqa)
                dma(SP, KA[3 + r3:4 + r3, :], parts3[i:i + 1, r3, :], [rp6], [rqa], rqa)
            if os.environ.get("FOXCUT") == "2":
                continue
            if os.environ.get("FOXCUT") == "3" and i >= 1:
                continue
            emit_softmax_attn(kb, S, [(QT[:, i, :], KT[:, i, :])], rq_fn, rk_fn, lambda kbk, i=i: V[:, kbk, i * 128:(i + 1) * 128],
                              SEQ, lambda qc: 4 * qc + 4, causal_mask_fn(S), negM, rnegM, i, extra=(QA, KA, rqa))
            emit_ag_o(kb, S, i)
        kb.release(m1)
    emit_mem_attn(kb, S, QmT, rq_fn, KmT, Vm, rkm, negMs[:, 4:5], rnegM)
    if getattr(S, "debug", False):
        dq = kb.dram("dbg_qm", [128, SEQ], BF16, "ExternalOutput")
        dk = kb.dram("dbg_km", [128, 256], BF16, "ExternalOutput")
        dv = kb.dram("dbg_vm", [128, 256], BF16, "ExternalOutput")
        rd = Res("dbgq")
        S.tracked.append(rd)
        dma(kb.SP, dq, QmT, [rT[t] for t in range(8)], [rd], rd)
        dma(kb.SP, dk, KmT, [rkm], [rd], rd)
        dma(kb.SP, dv, Vm.rearrange("p a b -> p (a b)"), [rkm], [rd], rd)


def build_program(layers=(0, 1, 2, 3), debug=False, stop=99):
    kb = KB()
    S = St()
    S.debug = debug
    import os
    S.exp_join = os.environ.get('EXPJOIN') == '1'
    S.no_down = os.environ.get('NO_DOWN') == '1'
    declare_io(kb, S, layers, stop)
    S_names = list(S.I.keys())
    emit_setup(kb, S)
    alloc_x(kb, S)
    emit_load_x(kb, S, S.I["xT"])
    nl = len(layers)
    for li, l in enumerate(layers):
        kind = KINDS[l % 4]
        if li == 0:
            emit_norm_out(kb, S, l, False)
            emit_spill_x(kb, S)
        kb.join(S.tracked)
        if stop <= 0:
            break
        emit_ag_hT(kb, S)
        if stop <= 1:
            break
        if kind in ("sb", "fox"):
            emit_phase_b_sbfox(kb, S, l, kind)
        elif kind == "swa":
            emit_phase_b_swa(kb, S, l)
        else:
            emit_phase_b_mla(kb, S, l)
        emit_ag_o_all(kb, S)
        kb.join(S.tracked)
        if stop <= 2:
            break
        if debug:
            dbg = kb.dram("dbg_o", [5, 512, SEQ], BF16, "ExternalOutput")
            rdbg = Res("dbg_o")
            S.tracked.append(rdbg)
            for i in range(5):
                dma(kb.SP, dbg[i], S.o_all[i], [S.r_o_all[i]], [rdbg], rdbg)
        if stop <= 3:
            break
        alloc_x(kb, S)
        emit_load_x(kb, S, S.xT_scr)
        if stop <= 4:
            break
        emit_outproj(kb, S, l)
        if stop <= 5:
            emit_norm_out(kb, S, -1, True)
            break
        emit_ffn(kb, S, l)
        if stop <= 6:
            break
        last = (li == nl - 1)
        if last and os.environ.get("DUMPX"):
            dbx = kb.dram("dbg_x", [D, TOK], F32, "ExternalOutput")
            rdbx = Res("dbg_x")
            S.tracked.append(rdbx)
            dxv = dbx.rearrange("(c p) t -> p c t", p=128)
            for t in range(2):
                dma(kb.SP, dxv[:, :, t * 512:(t + 1) * 512], S.xT[:, :, t * 512:(t + 1) * 512], [S.rX[t]], [rdbx], rdbx)
        if last:
            emit_norm_out(kb, S, -1, True)
        else:
            emit_norm_out(kb, S, layers[li + 1], False)
            emit_spill_x(kb, S)
    kb.final_sync()
    global LAST_NAMES
    LAST_NAMES = S_names
    return kb.finish()


def _tile_k(w):
    K, N = w.shape
    return np.ascontiguousarray(w.reshape(K // 128, 128, N).transpose(1, 0, 2))


def _t5_bucket_np(dist):
    max_exact = 16
    d = np.maximum(dist, 1).astype(np.float32)
    large = max_exact + (np.log(d / max_exact) / np.log(128 / max_exact) * (32 - max_exact)).astype(np.int32)
    return np.where(dist < max_exact, dist, np.minimum(large, 31))


def prep_inputs(inp, layers=(0, 1, 2, 3)):
    f32 = np.float32
    x = np.asarray(inp["x"], f32)
    mem = np.asarray(inp["mem"], f32)
    pos = np.asarray(inp["positions"]).astype(np.int32)
    shared = {}
    shared["cst"] = host_consts()
    gains = np.zeros((128, 4 * 3 * 16 + 16), f32)
    for l in range(4):
        for wi, nm in enumerate(["attn_norm", "ffn_norm", "mem_norm"]):
            g = np.asarray(inp[nm], f32)[l]
            gains[:, (l * 3 + wi) * 16:(l * 3 + wi + 1) * 16] = g.reshape(16, 128).T
    gains[:, 192:208] = np.asarray(inp["final_norm"], f32).reshape(16, 128).T
    shared["gains"] = gains
    percore = [dict() for _ in range(8)]
    for l in layers:
        kind = KINDS[l % 4]
        wout = np.asarray(inp[kind + "_w_out"], f32)[0]
        rows = []
        for r in range(4):
            for i in range(4):
                rows.append(np.arange((4 * r + i) * 128, (4 * r + i + 1) * 128))
            rows.append(np.arange(2048 + r * 128, 2048 + (r + 1) * 128))
        wperm = wout[np.concatenate(rows)]
        shared["wout%d" % l] = np.ascontiguousarray(wperm.reshape(20, 128, 16, 128).transpose(2, 1, 0, 3))
        wup = np.asarray(inp["ffn_w_up"], f32)[l]
        g_ = wup[:, :DFF].reshape(16, 128, NFF, 128)
        v_ = wup[:, DFF:].reshape(16, 128, NFF, 128)
        shared["wup%d" % l] = np.ascontiguousarray(np.concatenate([g_, v_], axis=3).transpose(2, 1, 0, 3))
        cw = np.asarray(inp["ffn_conv_w"], f32)[l]
        shared["convw%d" % l] = np.ascontiguousarray(cw.reshape(3, 2, NFF, 128).transpose(3, 2, 1, 0)).reshape(128, NFF * 6)
        cbv = np.asarray(inp["ffn_conv_b"], f32)[l]
        shared["convb%d" % l] = np.ascontiguousarray(cbv.reshape(2, NFF, 128).transpose(2, 1, 0)).reshape(128, NFF * 2)
        wd = np.asarray(inp["ffn_w_down"], f32)[l]
        shared["wdown%d" % l] = np.ascontiguousarray(wd.reshape(2, 22, 128, 16, 128).transpose(0, 3, 2, 1, 4))
        wmkv = np.asarray(inp["w_mem_kv"], f32)[l]
        win = np.asarray(inp[kind + "_w_in"], f32)[0]
        for j in range(4):
            wm = np.concatenate([wmkv[:, j * 128:(j + 1) * 128], wmkv[:, 512 + j * 128:512 + (j + 1) * 128]], axis=1)
            extra = {}
            if kind in ("sb", "fox"):
                base_m = 6144 if kind == "sb" else 6160
                cols = [win[:, j * 512:(j + 1) * 512], win[:, 2048 + j * 512:2048 + (j + 1) * 512],
                        win[:, 4096 + j * 512:4096 + (j + 1) * 512], win[:, base_m + j * 128: base_m + (j + 1) * 128]]
                if kind == "fox":
                    cols.append(win[:, 6144 + 4 * j: 6144 + 4 * j + 4])
                    extra["bf%d" % l] = np.ascontiguousarray(np.asarray(inp["fox_b_f"], f32)[0][4 * j:4 * j + 4].reshape(4, 1))
                wj = np.concatenate(cols, axis=1)
            elif kind == "swa":
                kcol = win[:, 2048 + j * 64:2048 + (j + 1) * 64]
                vcol = win[:, 2304 + j * 64:2304 + (j + 1) * 64]
                wj = np.concatenate([win[:, j * 512:(j + 1) * 512], kcol, kcol, vcol, win[:, 2560 + j * 128:2560 + (j + 1) * 128]], axis=1)
                rb = np.asarray(inp["rel_bias"], f32)
                s_ = np.arange(128)[:, None]
                t_ = np.arange(128)[None, :]
                hs = np.arange(8 * j, 8 * j + 8)
                dprev = 128 + t_ - s_
                dcur = t_ - s_
                bprev = np.where((dprev < 128)[None], rb[_t5_bucket_np(np.clip(dprev, 0, 127))][:, :, hs].transpose(2, 0, 1), NEG)
                bcur = np.where((dcur >= 0)[None], rb[_t5_bucket_np(np.clip(dcur, 0, 127))][:, :, hs].transpose(2, 0, 1), NEG)
                sw = np.stack([bprev.transpose(1, 0, 2), bcur.transpose(1, 0, 2)], axis=0)
                extra["swab%d" % l] = np.ascontiguousarray(sw.reshape(2, 128, 8 * 128)).astype(f32)
                extra["sinks%d" % l] = np.ascontiguousarray(np.asarray(inp["swa_sinks"], f32)[0][hs].reshape(1, 8))
                extra["relb%d" % l] = np.ascontiguousarray(rb[:, hs].T.reshape(1, 8 * 32))
            else:
                kr = win[:, 768:832]
                kr_sw = np.concatenate([kr[:, 32:64], kr[:, 0:32]], axis=1)
                wj = np.concatenate([win[:, 0:768], kr, kr_sw, win[:, 832 + j * 128:832 + (j + 1) * 128]], axis=1)
                wuq = np.asarray(inp["mla_w_uq"], f32)[0].reshape(512, 16, 192)
                hsel = wuq[:, 4 * j:4 * j + 4]
                qn = hsel[:, :, 0:128].reshape(512, 512)
                qp = hsel[:, :, 128:192]
                qps = np.concatenate([qp[:, :, 32:64], qp[:, :, 0:32]], axis=2)
                extra["wuq%d" % l] = _tile_k(np.concatenate([qn, qp.reshape(512, 256), qps.reshape(512, 256)], axis=1))
                wukv = np.asarray(inp["mla_w_ukv"], f32)[0].reshape(256, 16, 256)
                ksel = wukv[:, 4 * j:4 * j + 4]
                extra["wukv%d" % l] = _tile_k(np.concatenate([ksel[:, :, 0:128].reshape(256, 512), ksel[:, :, 128:256].reshape(256, 512)], axis=1))
                ln = np.zeros((128, 6), f32)
                ln[:, 0:4] = np.asarray(inp["mla_q_norm"], f32)[0].reshape(4, 128).T
                ln[:, 4:6] = np.asarray(inp["mla_kv_norm"], f32)[0].reshape(2, 128).T
                extra["lnorm%d" % l] = ln
            wj_t = _tile_k(np.ascontiguousarray(wj))
            wm_t = _tile_k(np.ascontiguousarray(wm))
            for b in range(2):
                c = b * 4 + j
                percore[c]["win%d" % l] = wj_t
                percore[c]["wmkv%d" % l] = wm_t
                percore[c].update(extra)
    inv_freq = (10000.0 ** (-np.arange(32, dtype=np.float32) / 32)).astype(f32)
    in_maps = []
    for c in range(8):
        b, j = c // 4, c % 4
        m = dict(shared)
        m.update(percore[c])
        m["xT"] = np.ascontiguousarray(x[b, j * TOK:(j + 1) * TOK, :].T)
        m["memT"] = np.ascontiguousarray(mem[b].T)
        m["pos"] = np.ascontiguousarray(pos[b].reshape(1, SEQ))
        r4 = np.arange(4)[None, :]
        p = np.arange(128)[:, None]
        m["idx_o"] = np.ascontiguousarray(((r4 * 128 + p) * 4 + j).astype(np.int32))
        c16 = np.arange(16)[None, :]
        jp = max(j - 1, 0)
        m["idx_h"] = np.ascontiguousarray((jp * D + c16 * 128 + p).astype(np.int32))
        misc = np.zeros((128, 8), f32)
        misc[:, 0] = 1.0 if j > 0 else 0.0
        misc[0:64, 1] = np.concatenate([inv_freq, inv_freq])
        misc[0:32, 2] = -1.0
        misc[32:64, 2] = 1.0
        m["misc"] = misc
        in_maps.append(m)
    return in_maps


_PROG_CACHE = {}
LAST_NAMES = []


def kernel(**inputs):
    layers = (0, 1, 2, 3)
    in_maps = prep_inputs(inputs, layers)
    nc = build_program(layers)
    res = run_bass_kernel_spmd(nc, in_maps, core_ids=list(range(8)))
    out = np.empty((2, SEQ, D), np.float32)
    for c in range(8):
        b, j = c // 4, c % 4
        out[b, j * TOK:(j + 1) * TOK, :] = res.results[c]["outT"].T
    return out


def emit_phase_b_swa(kb, S, l):
    I = S.I
    PE, ACT, DVE, POOL, SP = kb.PE, kb.ACT, kb.DVE, kb.POOL, kb.SP
    kb.release(S.base_top)
    KmT, Vm, rkm = emit_memkv(kb, S, l)
    QT = kb.alloc([4, SEQ], BF16, name="QT")
    KTa = kb.alloc([SEQ], BF16, name="KTa")
    KTb = kb.alloc([SEQ], BF16, name="KTb")
    rkz = Res("ktz")
    vec(kb.DVE, "memset", [], [rkz], ap=KTa, constant=0.0)
    vec(kb.DVE, "memset", [], [rkz], ap=KTb, constant=0.0)
    V = kb.alloc([32, 64], BF16, name="V")
    QmT = kb.alloc([SEQ], BF16, name="QmT")
    biasM = kb.alloc([2, 8 * 128], F32, name="biasM")
    rbias = Res("biasM")
    S.tracked.append(rbias)
    for kd in range(2):
        dma(SP, biasM[:, kd, :], I["swab%d" % l][kd], [], [rbias], rbias)
    small = kb.alloc([8 * 32 + 8 * 5], F32, name="swasmall")
    rsm = Res("swasmall")
    S.tracked.append(rsm)
    relb = small[:, 0:256]
    sinks = small[:, 256:264]
    maxb = small[:, 264:272]
    negMg = small[:, 272:280]
    esink = small[:, 280:288]
    dma(SP, relb, I["relb%d" % l].partition_broadcast(128), [], [rsm], rsm)
    dma(SP, sinks, I["sinks%d" % l].partition_broadcast(128), [], [rsm], rsm)
    rT = [Res("qkv%d" % t) for t in range(8)]
    m0 = kb.mark()
    W = kb.alloc([NCH, 832], BF16, name="W")
    rW = Res("W")
    S.tracked.append(rW)
    dma(POOL, W, I["win%d" % l], [], [rW], rW)
    hr = Ring(kb, [NCH, 512], BF16, 1, "hT")
    S.tracked += [r for _, r in hr.bufs]
    bk = Banks(kb, [0, 1, 2, 3])
    for tt in range(8):
        h, rh = hr.next()
        emit_load_hT(kb, S, h, rh, tt)
        ts = slice(tt * 512, (tt + 1) * 512)
        for oc in range(6):
            c0 = oc * 128 if oc < 5 else 704
            bank, rbank = bk.next()
            mm(kb, bank, [(W[:, c, c0:c0 + 128], h[:, c, :]) for c in range(NCH)], reads=[rW, rh], writes=[rbank])
            if oc < 4:
                act(kb, QT[:, oc, ts], bank, AF.Copy, reads=[rbank], writes=[rT[tt]], scale=64.0 ** -0.5)
            elif oc == 4:
                vec(DVE, "tensor_copy", reads=[rbank, rkz], writes=[rT[tt]], out=KTa[0:64, ts], in_=bank[0:64, :])
                vec(DVE, "tensor_copy", reads=[rbank, rkz], writes=[rT[tt]], out=KTb[64:128, ts], in_=bank[64:128, :])
            else:
                act(kb, QmT[:, ts], bank, AF.Copy, reads=[rbank], writes=[rT[tt]], scale=128.0 ** -0.5)
        for tb in range(4):
            bank, rbank = bk.next()
            mm(kb, bank[:, 0:64], [(h[:, c, tb * 128:(tb + 1) * 128], W[:, c, 640:704]) for c in range(NCH)],
               reads=[rW, rh], writes=[rbank])
            vec(DVE, "tensor_copy", reads=[rbank], writes=[rT[tt]], out=V[:, tt * 4 + tb, :], in_=bank[:, 0:64])
    kb.release(m0)
    rq_fn = lambda qc: [rT[qc]]
    rq_all = [rT[t] for t in range(8)]
    negMs = kb.alloc([8], F32, name="negMs")
    rnegM = Res("negMs")
    emit_mem_bound(kb, S, QmT, rq_all, KmT, rkm, negMs[:, 4:5], rnegM)
    emit_bound(kb, S, [QT[:, i, :] for i in range(4)], rq_all, SEQ, [KTa, KTb], rq_all, SEQ, negMs[:, 0:1], rnegM)
    vec(DVE, "tensor_reduce", reads=[rsm], writes=[rsm], out=maxb, in_=relb.rearrange("p (g b) -> p g b", g=8), axis=AX.X, op=ALU.max)
    vec(DVE, "tensor_scalar", reads=[rsm, rnegM], writes=[rsm], out=maxb, in0=maxb, scalar1=negMs[:, 0:1], scalar2=None, op0=ALU.subtract)
    vec(DVE, "tensor_tensor", reads=[rsm], writes=[rsm], out=maxb, in0=maxb, in1=sinks, op=ALU.max)
    vec(DVE, "tensor_scalar", reads=[rsm], writes=[rsm], out=negMg, in0=maxb, scalar1=-1.0, scalar2=None, op0=ALU.mult)
    vec(DVE, "tensor_tensor", reads=[rsm], writes=[rsm], out=esink, in0=sinks, in1=negMg, op=ALU.add)
    act(kb, esink, esink, AF.Exp, reads=[rsm], writes=[rsm])
    for kd in range(2):
        bv = biasM[:, kd, :].rearrange("p (g t) -> p g t", g=8)
        vec(DVE, "tensor_tensor", reads=[rbias, rsm], writes=[rbias], out=bv, in0=bv,
            in1=negMg.unsqueeze(2).to_broadcast([128, 8, 128]), op=ALU.add)
    kb.join(S.tracked)
    if os.environ.get("SWACUT") == "1":
        return
    m1 = kb.mark()
    lgr = Ring(kb, [512], F32, 2, "swl")
    pr = Ring(kb, [512], BF16, 3, "swp")
    dnr = Ring(kb, [512], F32, 2, "swd", parts=64)
    ostg = Ring(kb, [8, 512], BF16, 2, "swo", parts=64)
    S.tracked += [r for _, r in ostg.bufs]
    zb = Banks(kb, [0, 1])
    ob = Banks(kb, [2, 3])
    rb = Banks(kb, [4, 5])
    units = []
    for qb in range(32):
        for half in range(2):
            kinds = [1] if qb == 0 else [0, 1]
            for ki, kd in enumerate(kinds):
                units.append((qb, half, kd, ki == 0, ki == len(kinds) - 1))
    st = {}

    def stage_a(u):
        qb, half, kd, first, last = units[u]
        kbk = qb - 1 + kd
        Z, rZ = zb.next()
        items = []
        for hh in range(4):
            g = 4 * half + hh
            i_, e_ = g // 2, g % 2
            items.append((Z[:, hh * 128:(hh + 1) * 128], (KTa if e_ == 0 else KTb)[:, kbk * 128:(kbk + 1) * 128],
                          QT[:, i_, qb * 128:(qb + 1) * 128], True, True))
        mm_multi(kb, items, reads=[rT[qb // 4], rT[kbk // 4]], writes=[rZ])
        lg, rlg = lgr.next()
        vec(DVE, "tensor_tensor", reads=[rZ, rbias], writes=[rlg], out=lg, in0=Z,
            in1=biasM[:, kd, half * 512:(half + 1) * 512], op=ALU.add)
        P, rP = pr.next()
        act(kb, P, lg, AF.Exp, reads=[rlg], writes=[rP])
        st[u] = (P, rP)

    def stage_b(u):
        qb, half, kd, first, last = units[u]
        kbk = qb - 1 + kd
        P, rP = st.pop(u)
        if first:
            st["O"] = ob.next()
            st["R"] = rb.next()
        O, rO = st["O"]
        R, rR = st["R"]
        mm(kb, O[0:64, :], [(V[:, kbk, :], P)], reads=[rP, rT[kbk // 4]], writes=[rO], first=first, last=last)
        mm(kb, R[0:64, :], [(S.ones[:, 0:64], P)], reads=[rP, S.rcb], writes=[rR], first=first, last=last)
        if last:
            dn, rdn = dnr.next()
            vec(DVE, "tensor_tensor", reads=[rR, rsm], writes=[rdn], out=dn.rearrange("p (g t) -> p g t", g=4),
                in0=R[0:64, :].rearrange("p (g t) -> p g t", g=4),
                in1=esink[0:64, 4 * half:4 * half + 4].unsqueeze(2).to_broadcast([64, 4, 128]), op=ALU.add)
            vec(DVE, "reciprocal", reads=[rdn], writes=[rdn], out=dn, in_=dn)
            if qb % 4 == 0 and half == 0:
                st["og"] = ostg.next()
            og, rog = st["og"]
            vec(DVE, "tensor_tensor", reads=[rO, rdn], writes=[rog],
                out=og[:, 4 * half:4 * half + 4, (qb % 4) * 128:(qb % 4 + 1) * 128],
                in0=O[0:64, :].rearrange("p (g t) -> p g t", g=4), in1=dn.rearrange("p (g t) -> p g t", g=4), op=ALU.mult)
            if qb % 4 == 3 and half == 1:
                qc = qb // 4
                for i_ in range(4):
                    dv = S.o_loc[i_].rearrange("(e d) t -> d e t", e=2)
                    dma(SP, dv[:, :, qc * 512:(qc + 1) * 512], og[:, 2 * i_:2 * i_ + 2, :], [rog], [S.r_o_loc[i_]], S.r_o_loc[i_])

    n = len(units)
    if os.environ.get("SWACUT") == "2":
        n = 6
    stage_a(0)
    for u in range(n):
        if u + 1 < n:
            stage_a(u + 1)
        stage_b(u)
    for i_ in range(4):
        emit_ag_o(kb, S, i_)
    kb.release(m1)
    emit_mem_attn(kb, S, QmT, rq_fn, KmT, Vm, rkm, negMs[:, 4:5], rnegM)


TWO_PI_HI = 6.28125
TWO_PI_LO = 2.0 * np.pi - 6.28125


def emit_phase_b_mla(kb, S, l):
    I = S.I
    PE, ACT, DVE, POOL, SP = kb.PE, kb.ACT, kb.DVE, kb.POOL, kb.SP
    kb.release(S.base_top)
    KmT, Vm, rkm = emit_memkv(kb, S, l)
    cqn = kb.alloc([4, SEQ], BF16, name="cqn")
    ckvn = kb.alloc([2, SEQ], BF16, name="ckvn")
    KpeT = kb.alloc([SEQ], BF16, parts=64, name="KpeT")
    QmT = kb.alloc([SEQ], BF16, name="QmT")
    cos2 = kb.alloc([SEQ], F32, parts=64, name="cos2")
    ssin2 = kb.alloc([SEQ], F32, parts=64, name="ssin2")
    lnorm = kb.alloc([6], F32)
    rln = Res("lnorm")
    S.tracked.append(rln)
    dma(SP, lnorm, I["lnorm%d" % l], [], [rln], rln)
    rT = [Res("lat%d" % t) for t in range(8)]
    m0 = kb.mark()
    W = kb.alloc([NCH, 1024], BF16, name="W")
    rW = Res("W")
    S.tracked.append(rW)
    dma(POOL, W, I["win%d" % l], [], [rW], rW)
    hr = Ring(kb, [NCH, 512], BF16, 1, "hT")
    S.tracked += [r for _, r in hr.bufs]
    lat = kb.alloc([6, 512], F32, name="lat")
    rlat = Res("lat")
    sq = kb.alloc([4, 512], BF16)
    rsq = Res("lsq")
    rstd = kb.alloc([512], F32)
    rrstd = Res("lrstd")
    posi = kb.alloc([512], I32, parts=64)
    rpi = Res("posi")
    S.tracked.append(rpi)
    ang = kb.alloc([512], F32, parts=64)
    kf = kb.alloc([512], F32, parts=64)
    ki = kb.alloc([512], I32, parts=64)
    rang = Res("ang")
    t1 = kb.alloc([512], F32, parts=64)
    t2 = kb.alloc([512], F32, parts=64)
    rt12 = Res("t12")
    bk = Banks(kb, [0, 1, 2, 3])
    bk2 = Banks(kb, [4, 5])
    invf = S.misc[0:64, 1:2]
    sgn = S.misc[0:64, 2:3]
    rsave = S.rgain
    for tt in range(8):
        h, rh = hr.next()
        emit_load_hT(kb, S, h, rh, tt)
        ts = slice(tt * 512, (tt + 1) * 512)
        for oc in range(6):
            bank, rbank = bk.next()
            mm(kb, bank, [(W[:, c, oc * 128:(oc + 1) * 128], h[:, c, :]) for c in range(NCH)], reads=[rW, rh], writes=[rbank])
            if oc % 2 == 0:
                act(kb, lat[:, oc, :], bank, AF.Copy, reads=[rbank], writes=[rlat])
            else:
                vec(DVE, "tensor_copy", reads=[rbank], writes=[rlat], out=lat[:, oc, :], in_=bank)
        bank, rbank = bk.next()
        mm(kb, bank, [(W[:, c, 896:1024], h[:, c, :]) for c in range(NCH)], reads=[rW, rh], writes=[rbank])
        act(kb, QmT[:, ts], bank, AF.Copy, reads=[rbank], writes=[rT[tt]], scale=128.0 ** -0.5)
        S.rgain = rln
        b6, rb6 = kb.banks[6], kb.bank_res[6]
        emit_rmsnorm(kb, S, lambda c: lat[:, c, :], rlat, 4, 512, lnorm[:, 0:4], lambda c, ts=ts: cqn[:, c, ts], rT[tt],
                     sq, rsq, rstd, rrstd, b6, rb6, 512)
        emit_rmsnorm(kb, S, lambda c: lat[:, 4 + c, :], rlat, 2, 512, lnorm[:, 4:6], lambda c, ts=ts: ckvn[:, c, ts], rT[tt],
                     sq, rsq, rstd, rrstd, b6, rb6, 256)
        S.rgain = rsave
        dma(SP, posi, I["pos"][0:1, ts].partition_broadcast(64), [], [rpi], rpi)
        vec(DVE, "tensor_copy", reads=[rpi], writes=[rang], out=ang, in_=posi)
        vec(DVE, "tensor_scalar", reads=[rang, S.ridx], writes=[rang], out=ang, in0=ang, scalar1=invf, scalar2=None, op0=ALU.mult)
        for which, dst in ((0, ssin2), (1, cos2)):
            src = ang
            if which == 1:
                vec(DVE, "tensor_scalar", reads=[rang], writes=[rt12], out=t2, in0=ang, scalar1=float(np.pi / 2), scalar2=None, op0=ALU.add)
                src = t2
            vec(DVE, "tensor_scalar", reads=[rang, rt12], writes=[rt12], out=kf, in0=src, scalar1=float(1.0 / (2 * np.pi)), scalar2=None, op0=ALU.mult)
            vec(DVE, "tensor_copy", reads=[rt12], writes=[rt12], out=ki, in_=kf)
            vec(DVE, "tensor_copy", reads=[rt12], writes=[rt12], out=kf, in_=ki)
            vec(DVE, "scalar_tensor_tensor", reads=[rt12, rang], writes=[rt12], out=t1, in0=kf, scalar=-TWO_PI_HI, in1=src, op0=ALU.mult, op1=ALU.add)
            vec(DVE, "scalar_tensor_tensor", reads=[rt12], writes=[rt12], out=t1, in0=kf, scalar=-float(TWO_PI_LO), in1=t1, op0=ALU.mult, op1=ALU.add)
            act(kb, dst[:, ts], t1, AF.Sin, reads=[rt12], writes=[rT[tt]])
        vec(DVE, "tensor_scalar", reads=[rT[tt], S.ridx], writes=[rT[tt]], out=ssin2[:, ts], in0=ssin2[:, ts], scalar1=sgn, scalar2=None, op0=ALU.mult)
        ba, rba = bk2.next()
        bb, rbb = bk2.next()
        mm(kb, ba[0:64, :], [(W[:, c, 768:832], h[:, c, :]) for c in range(NCH)], reads=[rW, rh], writes=[rba])
        mm(kb, bb[0:64, :], [(W[:, c, 832:896], h[:, c, :]) for c in range(NCH)], reads=[rW, rh], writes=[rbb])
        vec(DVE, "tensor_tensor", reads=[rba, rT[tt]], writes=[rt12], out=t1, in0=ba[0:64, :], in1=cos2[:, ts], op=ALU.mult)
        vec(DVE, "tensor_tensor", reads=[rbb, rT[tt]], writes=[rt12], out=t2, in0=bb[0:64, :], in1=ssin2[:, ts], op=ALU.mult)
        vec(DVE, "tensor_tensor", reads=[rt12], writes=[rT[tt]], out=KpeT[:, ts], in0=t1, in1=t2, op=ALU.add)
    if os.environ.get("MLACUT") == "1":
        rd = Res("dbgm", True)
        S.tracked.append(rd)
        for nm, ap, shp in (("dbg_cqn", cqn.rearrange("p a b -> p (a b)"), [128, 4 * SEQ]), ("dbg_ckvn", ckvn.rearrange("p a b -> p (a b)"), [128, 2 * SEQ]),
                            ("dbg_kpe", KpeT, [64, SEQ])):
            dd = kb.dram(nm, shp, BF16, "ExternalOutput")
            dma(kb.SP, dd, ap, rT, [rd], rd)
        for nm, ap in (("dbg_cos", cos2), ("dbg_sin", ssin2)):
            dd = kb.dram(nm, [64, SEQ], F32, "ExternalOutput")
            dma(kb.SP, dd, ap, rT, [rd], rd)
        return
    kb.release(m0)
    rq_fn = lambda qc: [rT[qc]]
    rq_all = [rT[t] for t in range(8)]
    negMs = kb.alloc([8], F32, name="negMs")
    rnegM = Res("negMs")
    emit_mem_bound(kb, S, QmT, rq_all, KmT, rkm, negMs[:, 4:5], rnegM)
    Wuq = kb.alloc([4, 1024], BF16, name="Wuq")
    Wukv = kb.alloc([2, 1024], BF16, name="Wukv")
    rWu = Res("Wu")
    S.tracked.append(rWu)
    dma(POOL, Wuq, I["wuq%d" % l], [], [rWu], rWu)
    dma(POOL, Wukv, I["wukv%d" % l], [], [rWu], rWu)
    QnT = kb.alloc([SEQ], BF16, name="QnT")
    QpeT = kb.alloc([SEQ], BF16, parts=64, name="QpeT")
    KnT = kb.alloc([SEQ], BF16, name="KnT")
    Vh = kb.alloc([32, 128], BF16, name="Vh")
    t1 = kb.alloc([512], F32, parts=64)
    t2 = kb.alloc([512], F32, parts=64)
    rt12 = Res("t12b")
    sc = 192.0 ** -0.5
    rH = [Res("hq%d" % t) for t in range(8)]
    for i in range(4):
        for tt in range(8):
            ts = slice(tt * 512, (tt + 1) * 512)
            bank, rbank = bk.next()
            mm(kb, bank, [(Wuq[:, c, i * 128:(i + 1) * 128], cqn[:, c, ts]) for c in range(4)], reads=[rWu, rT[tt]], writes=[rbank])
            act(kb, QnT[:, ts], bank, AF.Copy, reads=[rbank], writes=[rH[tt]], scale=sc)
            bank, rbank = bk.next()
            mm(kb, bank, [(Wukv[:, c, i * 128:(i + 1) * 128], ckvn[:, c, ts]) for c in range(2)], reads=[rWu, rT[tt]], writes=[rbank])
            vec(DVE, "tensor_copy", reads=[rbank], writes=[rH[tt]], out=KnT[:, ts], in_=bank)
            ba, rba = bk2.next()
            bb, rbb = bk2.next()
            mm(kb, ba[0:64, :], [(Wuq[:, c, 512 + i * 64:512 + (i + 1) * 64], cqn[:, c, ts]) for c in range(4)], reads=[rWu, rT[tt]], writes=[rba])
            mm(kb, bb[0:64, :], [(Wuq[:, c, 768 + i * 64:768 + (i + 1) * 64], cqn[:, c, ts]) for c in range(4)], reads=[rWu, rT[tt]], writes=[rbb])
            vec(DVE, "tensor_tensor", reads=[rba, rT[tt]], writes=[rt12], out=t1, in0=ba[0:64, :], in1=cos2[:, ts], op=ALU.mult)
            vec(DVE, "tensor_tensor", reads=[rbb, rT[tt]], writes=[rt12], out=t2, in0=bb[0:64, :], in1=ssin2[:, ts], op=ALU.mult)
            vec(DVE, "tensor_tensor", reads=[rt12], writes=[rt12], out=t1, in0=t1, in1=t2, op=ALU.add)
            vec(DVE, "tensor_scalar", reads=[rt12], writes=[rH[tt]], out=QpeT[:, ts], in0=t1, scalar1=sc, scalar2=None, op0=ALU.mult)
            for tb in range(4):
                blk = tt * 4 + tb
                bank, rbank = bk.next()
                mm(kb, bank[:, 0:128], [(ckvn[:, c, blk * 128:(blk + 1) * 128], Wukv[:, c, 512 + i * 128:512 + (i + 1) * 128]) for c in range(2)],
                   reads=[rWu, rT[tt]], writes=[rbank])
                if tb % 2 == 0:
                    act(kb, Vh[:, blk, :], bank[:, 0:128], AF.Copy, reads=[rbank], writes=[rH[tt]])
                else:
                    vec(DVE, "tensor_copy", reads=[rbank], writes=[rH[tt]], out=Vh[:, blk, :], in_=bank[:, 0:128])
        emit_bound(kb, S, [QnT, QpeT], rH, SEQ, [KnT, KpeT], rH + rq_all, SEQ, negMs[:, i:i + 1], rnegM)
        kb.join(S.tracked)
        if os.environ.get("MLACUT") == "2":
            rd = Res("dbgm2", True)
            S.tracked.append(rd)
            for nm, ap, shp, dt in (("dbg_qn", QnT, [128, SEQ], BF16), ("dbg_qpe", QpeT, [64, SEQ], BF16), ("dbg_kn", KnT, [128, SEQ], BF16),
                                    ("dbg_vh", Vh.rearrange("p a b -> p (a b)"), [128, 32 * 128], BF16), ("dbg_negm", negMs, [128, 8], F32)):
                dd = kb.dram(nm, shp, dt, "ExternalOutput")
                dma(kb.SP, dd, ap, rH + [rnegM], [rd], rd)
            return
        emit_softmax_attn(kb, S, [(QnT, KnT), (QpeT, KpeT)], lambda qc, rH=rH: [rH[qc], rT[qc]],
                          lambda kbk, rH=rH: [rH[kbk // 4], rT[kbk // 4]], lambda kbk: Vh[:, kbk, :],
                          SEQ, lambda qc: 4 * qc + 4, causal_mask_fn(S), negMs[:, i:i + 1], rnegM, i)
        emit_ag_o(kb, S, i)
    emit_mem_attn(kb, S, QmT, rq_fn, KmT, Vm, rkm, negMs[:, 4:5], rnegM)
```
